# Optimizing a Trainium2 kernel written in Bass

```python
import jax, jax.numpy as jnp
from jax import lax
import numpy as np

D_MODEL = 1024
BATCH = 8
SEQ = 4096
DEPTH = 1

D_FF = 2816
D_A = D_MODEL
D_B = D_MODEL
GROUP = 128
CONV_A = 31
CONV_B = 3
EPS = 1e-6
SPLITS = (D_A, 2 * D_A, 2 * D_A + D_B, 2 * D_A + 2 * D_B, 2 * D_A + 3 * D_B, 2 * D_A + 3 * D_B + D_MODEL)
D_IN = 2 * D_A + 3 * D_B + 2 * D_MODEL

kernel_name = "macaron_gated_conformer_shortconv_hybrid"


def rmsnorm(x, g):
    xf = x.astype(jnp.float32)
    y = xf * lax.rsqrt(jnp.mean(xf * xf, axis=-1, keepdims=True) + EPS)
    return (y * g.astype(jnp.float32)).astype(x.dtype)


def layernorm(x, g, b):
    xf = x.astype(jnp.float32)
    mu = jnp.mean(xf, axis=-1, keepdims=True)
    var = jnp.mean(jnp.square(xf - mu), axis=-1, keepdims=True)
    y = (xf - mu) * lax.rsqrt(var + EPS)
    return (y * g.astype(jnp.float32) + b.astype(jnp.float32)).astype(x.dtype)


def swiglu(x, w_gate, w_up, w_down):
    return (jax.nn.silu(x @ w_gate) * (x @ w_up)) @ w_down


def causal_depthwise_conv(x, w):
    k, c = w.shape
    return lax.conv_general_dilated(
        x, w[:, None, :].astype(x.dtype), window_strides=(1,), padding=((k - 1, 0),),
        dimension_numbers=("NWC", "WIO", "NWC"), feature_group_count=c)


def setup_inputs(seed: int = 0) -> dict:
    key = jax.random.key(seed)
    ks = jax.random.split(key, 24)
    f32 = jnp.float32

    def nrm(k, shape, fan_in):
        return jax.random.normal(k, shape, f32) * (fan_in ** -0.5)

    def gain(k, shape):
        return 1.0 + 0.01 * jax.random.normal(k, shape, f32)

    L = DEPTH
    return {
        "x": jax.random.normal(ks[0], (BATCH, SEQ, D_MODEL), f32),
        "ffn1_norm": gain(ks[1], (L, D_MODEL)),
        "ffn1_w_gate": nrm(ks[2], (L, D_MODEL, D_FF), D_MODEL),
        "ffn1_w_up": nrm(ks[3], (L, D_MODEL, D_FF), D_MODEL),
        "ffn1_w_down": nrm(ks[4], (L, D_FF, D_MODEL), D_FF),
        "mix_norm": gain(ks[5], (L, D_MODEL)),
        "w_in": nrm(ks[6], (L, D_MODEL, D_IN), D_MODEL),
        "a_dw_w": nrm(ks[7], (L, CONV_A, D_A), CONV_A),
        "a_dw_b": 0.01 * jax.random.normal(ks[8], (L, D_A), f32),
        "a_ln_g": gain(ks[9], (L, D_A)),
        "a_ln_b": 0.01 * jax.random.normal(ks[10], (L, D_A), f32),
        "a_w_out": nrm(ks[11], (L, D_A, D_MODEL), D_A),
        "b_conv_w": nrm(ks[12], (L, CONV_B, D_B), CONV_B),
        "b_w_out": nrm(ks[13], (L, D_B, D_MODEL), D_B),
        "w_o": nrm(ks[14], (L, D_MODEL, D_MODEL), D_MODEL),
        "ffn2_norm": gain(ks[15], (L, D_MODEL)),
        "ffn2_w_gate": nrm(ks[16], (L, D_MODEL, D_FF), D_MODEL),
        "ffn2_w_up": nrm(ks[17], (L, D_MODEL, D_FF), D_MODEL),
        "ffn2_w_down": nrm(ks[18], (L, D_FF, D_MODEL), D_FF),
        "final_norm": gain(ks[19], (D_MODEL,)),
    }


def reference(x, ffn1_norm, ffn1_w_gate, ffn1_w_up, ffn1_w_down, mix_norm, w_in,
              a_dw_w, a_dw_b, a_ln_g, a_ln_b, a_w_out, b_conv_w, b_w_out, w_o,
              ffn2_norm, ffn2_w_gate, ffn2_w_up, ffn2_w_down, final_norm):
    h = x
    for l in range(DEPTH):
        h = h + 0.5 * swiglu(rmsnorm(h, ffn1_norm[l]), ffn1_w_gate[l], ffn1_w_up[l], ffn1_w_down[l])

        u = rmsnorm(h, mix_norm[l])
        z = u @ w_in[l]
        a_val, a_gate, b_B, b_C, b_x, g_a, g_b = jnp.split(z, SPLITS, axis=-1)

        a = a_val * jax.nn.sigmoid(a_gate)
        a = causal_depthwise_conv(a, a_dw_w[l]) + a_dw_b[l]
        a = jax.nn.silu(layernorm(a, a_ln_g[l], a_ln_b[l]))
        y_a = a @ a_w_out[l]

        v = causal_depthwise_conv(b_C * b_x, b_conv_w[l])
        y_b = (b_B * v) @ b_w_out[l]

        m = jax.nn.sigmoid(g_a) * y_a + jax.nn.sigmoid(g_b) * y_b
        h = h + m @ w_o[l]

        h = h + 0.5 * swiglu(rmsnorm(h, ffn2_norm[l]), ffn2_w_gate[l], ffn2_w_up[l], ffn2_w_down[l])
    return rmsnorm(h, final_norm)
```

```python
import numpy as np
import concourse.bass as bass
import concourse.mybir as mybir
from concourse.bass_utils import run_bass_kernel_spmd

F32 = mybir.dt.float32
BF16 = mybir.dt.bfloat16
ALU = mybir.AluOpType
AF = mybir.ActivationFunctionType

N_CORES = 8
D = 1024
KC = 8
SEQ = 4096
T = 512
NT = SEQ // T
FF = 2816
FC = FF // 128
CA = 31
CB = 3
HA = CA - 1
HB = CB - 1
EPS = 1e-6
D_IN = 7 * D

V_FFN1 = 0
V_MIX = 8
V_FFN2 = 16
V_FINAL = 24
V_ADWB = 32
V_ALNG = 40
V_ALNB = 48
V_ADW = 56
V_BCW = V_ADW + 8 * CA
V_ID = V_BCW + 8 * CB
NV = V_ID + 128
NP_PE_FIRST = 8
NP_PE_LAST = 12

SLAB = 2048


def slab_plan():
    plan = []
    for which in (1, 2):
        for f in range(FC):
            plan.append(("gu%d" % which, f, 2048))
        for d in range(2 * KC):
            plan.append(("dn%d" % which, d, 11 * 128))
        if which == 1:
            for s in range(20):
                plan.append(("win", s, 2048))
            for i in range(16):
                plan.append(("m3", i, 2048))
            for s in range(4):
                plan.append(("wo", s, 2048))
    return plan


PLAN = slab_plan()
NSLAB = len(PLAN)
SLAB_OFF = np.concatenate([[0], np.cumsum([p[2] for p in PLAN])]).astype(np.int64)
WTOT = int(SLAB_OFF[-1])
M1_SEQ = []
for _j in range(8):
    M1_SEQ += [_j, 8 + _j, 24 + _j, 32 + _j, 16 + _j]


def _kchunks(w, c0, ncols=128):
    kc = w.shape[0] // 128
    return w[:, c0:c0 + ncols].reshape(kc, 128, ncols).transpose(1, 0, 2)


def pack_weights(inp):
    out = np.empty((128, WTOT), np.float32)
    for n, (kind, idx, ne) in enumerate(PLAN):
        o = int(SLAB_OFF[n])
        if kind.startswith("gu"):
            wg = (inp["ffn1_w_gate"] if kind[2] == "1" else inp["ffn2_w_gate"])[0]
            wu = (inp["ffn1_w_up"] if kind[2] == "1" else inp["ffn2_w_up"])[0]
            parts = [_kchunks(wg, idx * 128).reshape(128, -1), _kchunks(wu, idx * 128).reshape(128, -1)]
            out[:, o:o + ne] = np.concatenate(parts, axis=1)
        elif kind.startswith("dn"):
            wd = (inp["ffn1_w_down"] if kind[2] == "1" else inp["ffn2_w_down"])[0]
            d, half = idx // 2, idx % 2
            full = _kchunks(wd, d * 128)
            out[:, o:o + ne] = full[:, half * 11:(half + 1) * 11, :].reshape(128, -1)
        elif kind == "win":
            w = inp["w_in"][0]
            parts = [_kchunks(w, M1_SEQ[2 * idx + ci] * 128).reshape(128, -1) for ci in range(2)]
            out[:, o:o + ne] = np.concatenate(parts, axis=1)
        elif kind == "m3":
            i, g = idx // 2, idx % 2
            if g == 0:
                parts = [_kchunks(inp["a_w_out"][0], i * 128).reshape(128, -1),
                         _kchunks(inp["b_w_out"][0], i * 128).reshape(128, -1)]
            else:
                parts = [_kchunks(inp["w_in"][0], (40 + i) * 128).reshape(128, -1),
                         _kchunks(inp["w_in"][0], (48 + i) * 128).reshape(128, -1)]
            out[:, o:o + ne] = np.concatenate(parts, axis=1)
        elif kind == "wo":
            w = inp["w_o"][0]
            parts = [_kchunks(w, (2 * idx + ci) * 128).reshape(128, -1) for ci in range(2)]
            out[:, o:o + ne] = np.concatenate(parts, axis=1)
        else:
            raise AssertionError(kind)
    return out


def pack_vecs(inp):
    v = np.zeros((128, NV), np.float32)

    def col(a):
        return np.asarray(a, np.float32).reshape(8, 128).T

    v[:, V_FFN1:V_FFN1 + 8] = col(inp["ffn1_norm"][0])
    v[:, V_MIX:V_MIX + 8] = col(inp["mix_norm"][0])
    v[:, V_FFN2:V_FFN2 + 8] = col(inp["ffn2_norm"][0])
    v[:, V_FINAL:V_FINAL + 8] = col(inp["final_norm"])
    v[:, V_ADWB:V_ADWB + 8] = col(inp["a_dw_b"][0])
    v[:, V_ALNG:V_ALNG + 8] = col(inp["a_ln_g"][0])
    v[:, V_ALNB:V_ALNB + 8] = col(inp["a_ln_b"][0])
    adw = np.asarray(inp["a_dw_w"][0], np.float32)
    v[:, V_ADW:V_ADW + 8 * CA] = adw.reshape(CA, 8, 128).transpose(2, 1, 0).reshape(128, 8 * CA)
    bcw = np.asarray(inp["b_conv_w"][0], np.float32)
    v[:, V_BCW:V_BCW + 8 * CB] = bcw.reshape(CB, 8, 128).transpose(2, 1, 0).reshape(128, 8 * CB)
    v[:, V_ID:V_ID + 128] = np.eye(128, dtype=np.float32)
    return v


class Buf:
    __slots__ = ("ap", "w", "r", "name", "gen")

    def __init__(self, ap, name=""):
        self.ap = ap
        self.w = None
        self.r = []
        self.name = name
        self.gen = 0


class View:
    __slots__ = ("ap", "bufs")

    def __init__(self, ap, bufs):
        self.ap = ap
        self.bufs = bufs


def vb(x):
    return list(x.bufs) if isinstance(x, View) else [x]


class Tracker:
    def __init__(self, nc, engines, sems):
        self.nc = nc
        self.eng = engines
        self.sem = sems
        self.cnt = {k: 0 for k in sems}
        self.waited = {e: {} for e in engines}
        self.n_wait = 0

    def _waits(self, eng, reads, writes):
        toks = {}
        for b in reads:
            if b.w is not None:
                toks[b.w[0]] = max(toks.get(b.w[0], 0), b.w[1])
        for b in writes:
            if b.w is not None:
                toks[b.w[0]] = max(toks.get(b.w[0], 0), b.w[1])
            for t in b.r:
                toks[t[0]] = max(toks.get(t[0], 0), t[1])
        E = self.eng[eng]
        wd = self.waited[eng]
        for sk in sorted(toks):
            val = toks[sk]
            if sk == "pe" and eng == "pe":
                continue
            if wd.get(sk, 0) < val:
                E.wait_ge(self.sem[sk], val)
                wd[sk] = val
                self.n_wait += 1

    def _record(self, tok, reads, writes):
        for b in reads:
            b.r.append(tok)
        for b in writes:
            b.w = tok
            b.r = []

    def op(self, eng, fn, reads=(), writes=(), mark=True):
        reads, writes = _unref(reads), _unref(writes)
        self._waits(eng, reads, writes)
        ins = fn(self.eng[eng])
        if mark:
            self.cnt[eng] += 1
            ins.then_inc(self.sem[eng], 1)
            tok = (eng, self.cnt[eng])
        else:
            tok = (eng, self.cnt[eng] + 1)
        self._record(tok, reads, writes)
        return tok

    def dma(self, queue, semkey, out_buf, in_buf, out_ap, in_ap, extra_reads=(), extra_writes=()):
        reads = _unref([in_buf] + list(extra_reads))
        writes = _unref([out_buf] + list(extra_writes))
        self._waits(queue, reads, writes)
        self.eng[queue].dma_start(out=out_ap, in_=in_ap).then_inc(self.sem[semkey], 16)
        self.cnt[semkey] += 16
        tok = (semkey, self.cnt[semkey])
        self._record(tok, reads, writes)
        return tok

    def wait_tok(self, eng, tok):
        if self.waited[eng].get(tok[0], 0) < tok[1]:
            self.eng[eng].wait_ge(self.sem[tok[0]], tok[1])
            self.waited[eng][tok[0]] = tok[1]


class NullTracker:
    def __init__(self):
        import collections
        self.cnt = collections.defaultdict(int)

    def op(self, eng, fn, reads=(), writes=(), mark=True):
        _unref(reads), _unref(writes)
        return None

    def dma(self, *a, **k):
        return None

    def wait_tok(self, *a, **k):
        return None


class Ref:
    __slots__ = ("buf", "gen", "ap")

    def __init__(self, buf, gen):
        self.buf = buf
        self.gen = gen
        self.ap = buf.ap


def _unref(items):
    out = []
    for b in items:
        if isinstance(b, Ref):
            assert b.gen == b.buf.gen, "stale ring buffer use: " + b.buf.name
            b = b.buf
        out.append(b)
    return out


class Ring:
    def __init__(self, bufs, name="ring"):
        self.bufs = bufs
        self.i = 0
        for q, b in enumerate(bufs):
            b.name = "%s%d" % (name, q)

    def next(self):
        b = self.bufs[self.i % len(self.bufs)]
        self.i += 1
        b.gen += 1
        return Ref(b, b.gen)


NSLOT = 10
NCAST = 4
HOLD_A = 6
DIRECT_CAST = True


class Builder:
    def __init__(self, nt=NT, debug_stage=None):
        self.nt = nt
        self.debug_stage = debug_stage
        nc = bass.Bass("TRN2", target_bir_lowering=False)
        self.nc = nc
        self.xT = nc.dram_tensor("xT", [D, SEQ], F32, kind="ExternalInput").ap()
        self.wpack = nc.dram_tensor("wpack", [128, WTOT], F32, kind="ExternalInput").ap()
        self.vecs_d = nc.dram_tensor("vecs", [128, NV], F32, kind="ExternalInput").ap()
        self.outT = nc.dram_tensor("outT", [D, SEQ], F32, kind="ExternalOutput").ap()
        self.wbf = nc.dram_tensor("wbf", [128, WTOT], BF16).ap()

    def sb(self, name, shape, dt):
        t = self.stack.enter_context(self.nc.sbuf_tensor(name, shape, dt))
        return t

    def build(self):
        from contextlib import ExitStack
        nc = self.nc
        with ExitStack() as st:
            self.stack = st
            sb = self.sb
            self.hT = [sb("hT%d" % i, [128, KC, T], F32) for i in range(2)]
            self.uF = sb("uF", [128, KC, T], BF16)
            self.uM = sb("uM", [128, KC, T], BF16)
            self.act = sb("act", [128, FC, T], BF16)
            self.acc = sb("acc", [128, KC, T], F32)
            self.acc1 = [sb("acc1_%d" % i, [128, T], F32) for i in range(2)]
            self.bact = sb("bact", [128, KC, T], BF16)
            self.aact = sb("aact", [128, KC, T], BF16)
            self.mm = sb("mm", [128, KC, T], BF16)
            self.abuf = [sb("abuf%d" % i, [128, HA + T], F32) for i in range(3)]
            self.ab16 = [sb("ab16_%d" % i, [128, HA + T + 2], BF16) for i in range(1)]
            self.diag = [sb("diag%d" % i, [128, 128], BF16) for i in range(4)]
            self.cxb = [sb("cx%d" % i, [128, HB + T], F32) for i in range(2)]
            self.ahist = sb("ahist", [128, KC, HA], F32)
            self.bhist = sb("bhist", [128, KC, HB], F32)
            self.sq = [sb("sq%d" % i, [128, T], BF16) for i in range(2)]
            self.accb = [sb("accb%d" % i, [128, T], BF16) for i in range(2)]
            self.sqL = [sb("sqL%d" % i, [128, T], BF16) for i in range(2)]
            self.fpool = [sb("fp%d" % i, [128, T], F32) for i in range(7)]
            self.thb = [sb("thg%d" % i, [128, T], F32) for i in range(4)]
            self.rstdF = sb("rstdF", [128, T], F32)
            self.rstdM = sb("rstdM", [128, T], F32)
            self.meanb = sb("meanb", [128, T], F32)
            self.lnr = sb("lnr", [128, T], F32)
            self.nmr = sb("nmr", [128, T], F32)
            self.vecs = sb("vecs_sb", [128, NV], F32)
            self.ones = sb("ones", [128, 128], BF16)
            self.slots = [sb("wslot%d" % i, [128, SLAB], BF16) for i in range(NSLOT)]
            self.psum = [st.enter_context(nc.psum_tensor("ps%d" % i, [128, T], F32)) for i in range(8)]
            semnames = ["pe", "act", "dve", "pool"] + ["slot%d" % i for i in range(NSLOT)] + \
                       ["cslot%d" % i for i in range(NSLOT)] + \
                       ["cast%d" % i for i in range(NCAST)] + ["xld0", "xld1", "xld2", "ost0", "ost1", "misc"]
            sems = {k: st.enter_context(nc.semaphore(k)) for k in semnames}
            block = st.enter_context(nc.Block())
            self.block = block
            engs = {}
            self._emit_all(block, sems)
        return nc

    def _emit_all(self, block, sems):
        nc = self.nc
        rec = {"pe": [], "act": [], "dve": [], "pool": [], "sp": []}

        class Proxy:
            def __init__(self, name):
                self.name = name

            def __getattr__(self, meth):
                lst = rec[self.name]

                def call(*a, **kw):
                    entry = [meth, a, kw, []]
                    lst.append(entry)

                    class H:
                        def then_inc(_s, sem, val=1):
                            entry[3].append((sem, val))
                            return _s
                    return H()
                return call

        engines = {k: Proxy(k) for k in rec}
        self.tr = Tracker(nc, engines, sems)
        self._program()

        def replay(name):
            def body(E):
                for meth, a, kw, incs in rec[name]:
                    ins = getattr(E, meth)(*a, **kw)
                    for sem, val in incs:
                        ins.then_inc(sem, val)
            return body

        block.tensor(replay("pe"))
        block.scalar(replay("act"))
        block.vector(replay("dve"))
        block.gpsimd(replay("pool"))
        block.sync(replay("sp"))

    def _mkbufs(self):
        B = Buf
        self.b_h = [[B(self.hT[p][:, k, :], "h%d_%d" % (p, k)) for k in range(KC)] for p in range(2)]
        self.b_uF = [B(self.uF[:, k, :]) for k in range(KC)]
        self.b_uM = [B(self.uM[:, k, :]) for k in range(KC)]
        self.b_act = [B(self.act[:, f, :]) for f in range(FC)]
        self.b_acc = [B(self.acc[:, k, :]) for k in range(KC)]
        self.b_acc1 = [B(t[:]) for t in self.acc1]
        self.b_bact = [B(self.bact[:, k, :]) for k in range(KC)]
        self.b_aact = [B(self.aact[:, k, :]) for k in range(KC)]
        self.b_mm = [B(self.mm[:, k, :]) for k in range(KC)]
        self.r_abuf = Ring([B(t) for t in self.abuf], "abuf")
        self.r_ab16 = Ring([B(t) for t in self.ab16], "ab16")
        self.r_diag = Ring([B(t[:]) for t in self.diag], "diag")
        self.r_cx = Ring([B(t) for t in self.cxb], "cx")
        self.b_ahist = [B(self.ahist[:, k, :]) for k in range(KC)]
        self.b_bhist = [B(self.bhist[:, k, :]) for k in range(KC)]
        self.r_sq = Ring([B(t[:]) for t in self.sq], "sq")
        self.r_accb = Ring([B(t[:]) for t in self.accb], "accb")
        self.r_sqL = Ring([B(t[:]) for t in self.sqL], "sqL")
        self.r_f = Ring([B(t[:]) for t in self.fpool], "fpool")
        self.r_th = Ring([B(t[:]) for t in self.thb], "thg")
        self.b_rstdF = B(self.rstdF[:])
        self.b_rstdM = B(self.rstdM[:])
        self.b_mean = B(self.meanb[:])
        self.b_lnr = B(self.lnr[:])
        self.b_nmr = B(self.nmr[:])
        self.b_vecs = B(self.vecs[:])
        self.b_ones = B(self.ones[:])
        self.b_slot = [B(t) for t in self.slots]
        self.r_ps = Ring([B(self.psum[i][:]) for i in range(6)], "psum")
        self.b_S0 = B(self.psum[6][:])
        self.b_S1 = B(self.psum[7][:])
        self.b_wsrc = B(None, "wpack")
        self.b_scr = {(p[0], p[1]): B(None, "scr") for p in PLAN}
        self.b_xsrc = B(None, "xT")
        self.b_out = B(None, "outT")

    def _program(self):
        real_tr = self.tr
        self.tr = NullTracker()
        self.dry = True
        self.reqs = []
        self._mkbufs()
        self._body()
        reqs = self.reqs
        self.loads = []
        self.last_req = {}
        seen = set()
        for r, key in enumerate(reqs):
            if key not in seen:
                seen.add(key)
                self.loads.append(key)
            self.last_req[key] = r
        self.tr = real_tr
        self.dry = False
        self.req_i = 0
        self.next_load = 0
        self.slot_key = [None] * NSLOT
        self.slot_of = {}
        self.free_q = list(range(NSLOT))
        self._mkbufs()
        self._setup()
        self._body()
        tr = self.tr
        for p in range(2):
            k = "ost%d" % p
            if tr.cnt[k]:
                tr.wait_tok("sp", (k, tr.cnt[k]))

    def _setup(self):
        tr = self.tr
        B = Buf
        tr.dma("sp", "misc", self.b_vecs, B(None), self.vecs[:], self.vecs_d[:])
        tr.op("dve", lambda E: E.memset(self.ones[:], 1.0 / D), writes=[self.b_ones])
        tr.op("dve", lambda E: E.memset(self.ahist[:], 0.0), writes=self.b_ahist)
        tr.op("dve", lambda E: E.memset(self.bhist[:], 0.0), writes=self.b_bhist)
        tr.op("dve", lambda E: E.tensor_scalar_mul(out=self.vecs[:, V_ADW:V_ADW + 8 * CA],
                                                   in0=self.vecs[:, V_ADW:V_ADW + 8 * CA], scalar1=0.5),
              reads=[self.b_vecs], writes=[self.b_vecs])
        if self.nt > 1:
            self.load_x_staged(1)
        self.pidx = {(p[0], p[1]): n for n, p in enumerate(PLAN)}
        self.cast_seen = set()
        self.slot_dirty = [None] * NSLOT
        self.nwb = 0
        if DIRECT_CAST:
            return
        order = []
        seen = set()
        for key in self.loads:
            k2 = (key[1], key[2])
            if k2 not in seen:
                seen.add(k2)
                order.append(k2)
        assert len(order) == NSLAB
        pidx = {(p[0], p[1]): n for n, p in enumerate(PLAN)}
        for q, k2 in enumerate(order):
            n = pidx[k2]
            o, ne = int(SLAB_OFF[n]), PLAN[n][2]
            sk = "cast%d" % (q % NCAST)
            if q >= NCAST:
                tr.wait_tok("pool", (sk, 16 * (q // NCAST)))
            tr.dma("pool", sk, self.b_scr[k2], self.b_wsrc, self.wbf[:, o:o + ne], self.wpack[:, o:o + ne])
        self.pidx = pidx

    def _body(self):
        nt = self.nt
        self.load_x(0)
        for _ in self.gen_ffn(0, self.b_h[0], 1):
            pass
        self.mixer_norm(self.b_h[0])
        HOLD = 10
        for i in range(nt):
            h = self.b_h[i % 2]
            if i >= 1 and i + 1 < nt:
                self.load_x_staged(i + 1)
            if i == 0:
                nb = 31 if nt > 1 else 0
            elif i + 1 < nt:
                nb = 62
            else:
                nb = 31
            gb = self.gen_side(i) if nb else None
            self.merge(self.gen_m1(i, h), 40, gb, max(nb - HOLD, 0), delay_b=1, stop_b=max(nb - HOLD, 0))
            self.ln_m3_m4(i, h, (self.b_h[(i + 1) % 2] if i + 1 < nt else None), gb)
        hl = self.b_h[(nt - 1) % 2]
        for _ in self.gen_ffn_f(nt - 1, 2):
            pass
        for _ in self.gen_ffn_d(nt - 1, hl, 2, hl):
            pass
        self.final_store(nt - 1, hl)

    def gen_side(self, i):
        nt = self.nt
        views, _, _ = self.x_staging()
        hn = self.b_h[(i + 1) % 2]
        if i == 0:
            self.ffn_norm(views, 1)
            yield
            yield from self.gen_ffn_f(1, 1)
            yield from self.gen_ffn_d(1, hn, 1, views)
            return
        hp = self.b_h[(i - 1) % 2]
        yield from self.gen_ffn_f(i - 1, 2)
        if i + 1 < nt:
            self.ffn_norm(views, 1)
            yield
        yield from self.gen_ffn_d(i - 1, hp, 2, hp)
        if i + 1 < nt:
            gf = self.gen_ffn_f(i + 1, 1)
            next(gf)
            yield
            next(gf)
            yield
            self.final_store(i - 1, hp)
            yield
            yield from gf
            yield from self.gen_ffn_d(i + 1, hn, 1, views)
        else:
            self.final_store(i - 1, hp)
            yield

    @staticmethod
    def merge(ga, na, gb, nb, delay_b=0, stop_b=None):
        da = db = 0
        a_live, b_live = True, (gb is not None and (stop_b is None or stop_b > 0))
        while a_live or b_live:
            pick_a = a_live and (not b_live or da < delay_b or da * nb <= db * na)
            if pick_a:
                try:
                    next(ga)
                    da += 1
                except StopIteration:
                    a_live = False
            else:
                try:
                    next(gb)
                    db += 1
                    if stop_b is not None and db >= stop_b:
                        b_live = False
                except StopIteration:
                    b_live = False

    def load_x(self, i):
        p = i % 2
        src = self.xT.rearrange("(k p) t -> p k t", p=128)[:, :, i * T:(i + 1) * T]
        self.tr.dma("sp", "xld2", self.b_h[p][0], self.b_xsrc, self.hT[p][:, :, :], src,
                    extra_writes=self.b_h[p][1:])

    def x_staging(self):
        a = self.aact[:].rearrange("p k t -> p (k t)").bitcast(F32)
        m = self.mm[:].rearrange("p k t -> p (k t)").bitcast(F32)
        views = []
        for k in range(KC):
            t, bufs = (a, self.b_aact) if k < 4 else (m, self.b_mm)
            kk = k % 4
            views.append(View(t[:, kk * T:(kk + 1) * T], [bufs[2 * kk], bufs[2 * kk + 1]]))
        return views, a, m

    def load_x_staged(self, i):
        _, a, m = self.x_staging()
        xs = self.xT.rearrange("(k p) t -> p k t", p=128)
        self.tr.dma("pool", "xld0", self.b_aact[0], self.b_xsrc, a.rearrange("p (k t) -> p k t", k=4),
                    xs[:, 0:4, i * T:(i + 1) * T], extra_writes=self.b_aact[1:])
        self.tr.dma("pool", "xld1", self.b_mm[0], self.b_xsrc, m.rearrange("p (k t) -> p k t", k=4),
                    xs[:, 4:8, i * T:(i + 1) * T], extra_writes=self.b_mm[1:])

    def store_out(self, i):
        p = i % 2
        dst = self.outT.rearrange("(k p) t -> p k t", p=128)[:, :, i * T:(i + 1) * T]
        self.tr.dma("sp", "ost%d" % p, self.b_out, self.b_h[p][0], dst, self.hT[p][:, :, :],
                    extra_reads=self.b_h[p][1:])

    def slab(self, tile, kind, idx):
        key = (tile, kind, idx)
        if self.dry:
            if not self.reqs or self.reqs[-1] != key:
                self.reqs.append(key)
            return self.b_slot[0], self.slots[0]
        tr = self.tr
        if self.req_i > 0 and self.reqs[self.req_i - 1] == key:
            s = self.slot_of[key]
            return self.b_slot[s], self.slots[s]
        r = self.req_i
        assert self.reqs[r] == key, (self.reqs[r], key)
        self.req_i += 1
        for s in range(NSLOT):
            k2 = self.slot_key[s]
            if k2 is not None and self.last_req[k2] < r - 1:
                self.slot_key[s] = None
                del self.slot_of[k2]
                self.free_q.append(s)
                if self.slot_dirty[s] is not None:
                    self._writeback(s)
        while self.next_load < len(self.loads) and self.free_q:
            s = self.free_q.pop(0)
            k2 = self.loads[self.next_load]
            kk = (k2[1], k2[2])
            n = self.pidx[kk]
            o, ne = int(SLAB_OFF[n]), PLAN[n][2]
            if DIRECT_CAST and kk not in self.cast_seen:
                self.cast_seen.add(kk)
                tr.dma("pool", "cslot%d" % s, self.b_slot[s], self.b_wsrc,
                       self.slots[s][:, 0:ne], self.wpack[:, o:o + ne])
                self.slot_dirty[s] = kk
            else:
                if kk in self.slot_dirty:
                    self._writeback(self.slot_dirty.index(kk))
                tr.dma("sp", "slot%d" % s, self.b_slot[s], self.b_scr[kk],
                       self.slots[s][:, 0:ne], self.wbf[:, o:o + ne])
            self.slot_key[s] = k2
            self.slot_of[k2] = s
            self.next_load += 1
        assert key in self.slot_of, ("weight slab not resident", key)
        s = self.slot_of[key]
        return self.b_slot[s], self.slots[s]

    def _writeback(self, s):
        tr = self.tr
        kk = self.slot_dirty[s]
        n = self.pidx[kk]
        o, ne = int(SLAB_OFF[n]), PLAN[n][2]
        q = self.nwb
        sk = "cast%d" % (q % NCAST)
        if q >= NCAST:
            tr.wait_tok("sp", (sk, 16 * (q // NCAST)))
        tr.dma("sp", sk, self.b_scr[kk], self.b_slot[s], self.wbf[:, o:o + ne], self.slots[s][:, 0:ne])
        self.nwb += 1
        self.slot_dirty[s] = None

    def mm_group(self, ps, pairs):
        tr = self.tr
        n = len(pairs)
        for q, (lap, lbuf, rbuf) in enumerate(pairs):
            tr.op("pe", lambda E, lap=lap, rbuf=rbuf, q=q: E.matmul(
                ps.ap, lhsT=lap, rhs=rbuf.ap, start=(q == 0), stop=(q == n - 1)),
                reads=[lbuf, rbuf], writes=([ps] if q == 0 else []), mark=(q == n - 1))
        (ps.buf if isinstance(ps, Ref) else ps).w = ("pe", tr.cnt["pe"])

    def norm_stats(self, h, rstd):
        tr = self.tr
        ps = self.r_ps.next()
        for k in range(KC):
            sq = self.r_sq.next()
            tr.op("act", lambda E, k=k, sq=sq: E.activation(out=sq.ap, in_=h[k].ap, func=AF.Square),
                  reads=vb(h[k]), writes=[sq])
            tr.op("pe", lambda E, k=k, sq=sq: E.matmul(ps.ap, lhsT=self.ones[:], rhs=sq.ap,
                                                        start=(k == 0), stop=(k == KC - 1)),
                  reads=[self.b_ones, sq], writes=([ps] if k == 0 else []), mark=True)
        ps.buf.w = ("pe", tr.cnt["pe"])
        sd = self.r_f.next()
        tr.op("act", lambda E: E.activation(out=sd.ap, in_=ps.ap, func=AF.Sqrt, bias=EPS, scale=1.0),
              reads=[ps], writes=[sd])
        tr.op("dve", lambda E: E.reciprocal(out=rstd.ap, in_=sd.ap), reads=[sd], writes=[rstd])
        return rstd

    def normalize(self, h, u, vcol, rstd):
        tr = self.tr
        for k in range(KC):
            tr.op("dve", lambda E, k=k: E.scalar_tensor_tensor(
                out=u[k].ap, in0=h[k].ap, scalar=self.vecs[:, vcol + k:vcol + k + 1], in1=rstd.ap,
                op0=ALU.mult, op1=ALU.mult), reads=vb(h[k]) + [rstd, self.b_vecs], writes=[u[k]])

    def mixer_norm(self, h):
        rstd = self.norm_stats(h, self.b_rstdM)
        self.normalize(h, self.b_uM, V_MIX, rstd)

    def ffn_norm(self, src, which):
        vcol = V_FFN1 if which == 1 else V_FFN2
        rstd = self.norm_stats(src, self.b_rstdF)
        self.normalize(src, self.b_uF, vcol, rstd)

    def gen_ffn_f(self, tile, which):
        tr = self.tr
        u = self.b_uF
        gk = "gu%d" % which
        for f in range(FC):
            sbuf, st = self.slab(tile, gk, f)
            psg = self.r_ps.next()
            self.mm_group(psg, [(st[:, k * 128:(k + 1) * 128], sbuf, u[k]) for k in range(KC)])
            psu = self.r_ps.next()
            self.mm_group(psu, [(st[:, (8 + k) * 128:(8 + k + 1) * 128], sbuf, u[k]) for k in range(KC)])
            sg = self.r_f.next()
            tr.op("act", lambda E, sg=sg, psg=psg: E.activation(out=sg.ap, in_=psg.ap, func=AF.Silu),
                  reads=[psg], writes=[sg])
            a = self.b_act[f]
            tr.op("dve", lambda E, sg=sg, psu=psu, a=a: E.tensor_tensor(out=a.ap, in0=sg.ap, in1=psu.ap, op=ALU.mult),
                  reads=[sg, psu], writes=[a])
            yield

    def gen_ffn_d(self, tile, h, which, src):
        tr = self.tr
        dk = "dn%d" % which
        for d in range(KC):
            sb0, st0 = self.slab(tile, dk, 2 * d)
            sb1, st1 = self.slab(tile, dk, 2 * d + 1)
            psd = self.r_ps.next()
            self.mm_group(psd, [((st0 if f < 11 else st1)[:, (f % 11) * 128:(f % 11 + 1) * 128],
                                 (sb0 if f < 11 else sb1), self.b_act[f]) for f in range(FC)])
            tr.op("dve", lambda E, d=d, psd=psd: E.scalar_tensor_tensor(
                out=h[d].ap, in0=psd.ap, scalar=0.5, in1=src[d].ap, op0=ALU.mult, op1=ALU.add),
                reads=[psd] + vb(src[d]), writes=[h[d]])
            yield

    def gen_ffn(self, tile, h, which, xin=None):
        src = xin if xin is not None else h
        self.ffn_norm(src, which)
        yield
        yield from self.gen_ffn_f(tile, which)
        yield from self.gen_ffn_d(tile, h, which, src)

    def win_group(self, tile, c, u):
        sbuf, st = self.slab(tile, "win", c // 2)
        ci = c % 2
        ps = self.r_ps.next()
        self.mm_group(ps, [(st[:, (ci * 8 + k) * 128:(ci * 8 + k + 1) * 128], sbuf, u[k]) for k in range(KC)])
        return ps

    def ln_stat_mm(self, j, ab16, sq):
        tr = self.tr
        tr.op("pe", lambda E: E.matmul(self.b_S0.ap, lhsT=self.ones[:], rhs=ab16.ap,
                                       start=(j == 0), stop=(j == KC - 1)),
              reads=[self.b_ones, ab16], writes=([self.b_S0] if j == 0 else []), mark=False)
        tr.op("pe", lambda E: E.matmul(self.b_S1.ap, lhsT=self.ones[:], rhs=sq.ap,
                                       start=(j == 0), stop=(j == KC - 1)),
              reads=[self.b_ones, sq], writes=([self.b_S1] if j == 0 else []), mark=True)
        self.b_S0.w = ("pe", tr.cnt["pe"])
        self.b_S1.w = ("pe", tr.cnt["pe"])

    def gen_m1(self, tile, h):
        tr = self.tr
        u = self.b_uM
        V = self.vecs

        def head(j):
                ps_av = self.win_group(tile, 5 * j + 0, u)
                ps_ag = self.win_group(tile, 5 * j + 1, u)
                th = self.r_f.next()
                tr.op("act", lambda E, th=th, ps=ps_ag: E.activation(out=th.ap, in_=ps.ap, func=AF.Tanh, scale=0.5),
                      reads=[ps_ag], writes=[th])
                ab = self.r_abuf.next()
                abt = ab.ap
                tr.op("act", lambda E, abt=abt, j=j: E.activation(out=abt[:, 0:HA], in_=self.ahist[:, j, :], func=AF.Copy),
                      reads=[self.b_ahist[j]], writes=[ab])
                tr.op("dve", lambda E, abt=abt, th=th, ps=ps_av: E.scalar_tensor_tensor(
                    out=abt[:, HA:HA + T], in0=th.ap, scalar=1.0, in1=ps.ap, op0=ALU.add, op1=ALU.mult),
                    reads=[th, ps_av], writes=[ab])
                ps_bc = self.win_group(tile, 5 * j + 2, u)
                ps_bx = self.win_group(tile, 5 * j + 3, u)
                ps_bb = self.win_group(tile, 5 * j + 4, u)
                bx = self.r_f.next()
                tr.op("act", lambda E, bx=bx, ps=ps_bx: E.activation(out=bx.ap, in_=ps.ap, func=AF.Copy),
                      reads=[ps_bx], writes=[bx])
                cx = self.r_cx.next()
                cxt = cx.ap
                tr.op("act", lambda E, cxt=cxt, j=j: E.activation(out=cxt[:, 0:HB], in_=self.bhist[:, j, :], func=AF.Copy),
                      reads=[self.b_bhist[j]], writes=[cx])
                tr.op("dve", lambda E, cxt=cxt, bx=bx, ps=ps_bc: E.tensor_tensor(
                    out=cxt[:, HB:HB + T], in0=bx.ap, in1=ps.ap, op=ALU.mult), reads=[bx, ps_bc], writes=[cx])
                A = [self.b_acc[j], self.b_acc1[j % 2]]
                wa = V_ADW + j * CA
                wc = V_BCW + j * CB
                v = self.r_f.next()

                def tap(k):
                    a = A[k % 2]
                    if k == 0:
                        tr.op("dve", lambda E: E.tensor_scalar(
                            out=a.ap, in0=abt[:, 0:T], scalar1=V[:, wa:wa + 1], scalar2=V[:, V_ADWB + j:V_ADWB + j + 1],
                            op0=ALU.mult, op1=ALU.add), reads=[ab, self.b_vecs], writes=[a])
                    elif k == 1:
                        tr.op("dve", lambda E: E.tensor_scalar_mul(out=a.ap, in0=abt[:, 1:1 + T], scalar1=V[:, wa + 1:wa + 2]),
                              reads=[ab, self.b_vecs], writes=[a])
                    else:
                        tr.op("dve", lambda E: E.scalar_tensor_tensor(
                            out=a.ap, in0=abt[:, k:k + T], scalar=V[:, wa + k:wa + k + 1], in1=a.ap, op0=ALU.mult, op1=ALU.add),
                            reads=[ab, a, self.b_vecs], writes=[a])

                def btap(k):
                    if k == 0:
                        tr.op("dve", lambda E: E.tensor_scalar_mul(out=v.ap, in0=cxt[:, 0:T], scalar1=V[:, wc:wc + 1]),
                              reads=[cx, self.b_vecs], writes=[v])
                    else:
                        tr.op("dve", lambda E: E.scalar_tensor_tensor(
                            out=v.ap, in0=cxt[:, k:k + T], scalar=V[:, wc + k:wc + k + 1], in1=v.ap, op0=ALU.mult, op1=ALU.add),
                            reads=[cx, v, self.b_vecs], writes=[v])
                tap(0)
                btap(0)
                tap(1)
                btap(1)
                tr.op("dve", lambda E, abt=abt, j=j: E.tensor_copy(out=self.ahist[:, j, :], in_=abt[:, T:T + HA]),
                      reads=[ab], writes=[self.b_ahist[j]])
                tap(2)
                btap(2)
                tr.op("dve", lambda E, cxt=cxt, j=j: E.tensor_copy(out=self.bhist[:, j, :], in_=cxt[:, T:T + HB]),
                      reads=[cx], writes=[self.b_bhist[j]])
                tap(3)
                tr.op("dve", lambda E, v=v, ps=ps_bb, j=j: E.tensor_tensor(out=self.b_bact[j].ap, in0=v.ap, in1=ps.ap, op=ALU.mult),
                      reads=[v, ps_bb], writes=[self.b_bact[j]])

                return (j, tap, A, ab, wa)

        pend_act = None
        pend_mm = None
        ctx = head(0)
        yield
        np_pe = NP_PE_FIRST if tile == 0 else (NP_PE_LAST if tile == self.nt - 1 else 0)
        K_DVE = CA - np_pe
        nd = K_DVE - 4
        segs = [(4, 4 + nd // 4), (4 + nd // 4, 4 + nd // 2), (4 + nd // 2, 4 + (3 * nd) // 4), (4 + (3 * nd) // 4, K_DVE)]
        for j in range(KC):
            _, tap, A, ab, wa = ctx
            nxt = None
            R = None
            for si, (k0, k1) in enumerate(segs):
                for k in range(k0, k1 - (1 if si == 3 else 0)):
                    tap(k)
                if si == 2 and np_pe:
                    R = self.pe_taps(ab, wa, K_DVE)
                if si == 3:
                    if np_pe:
                        a1 = A[1]
                        tr.op("dve", lambda E, a1=a1, R=R: E.tensor_tensor(out=a1.ap, in0=a1.ap, in1=R.ap, op=ALU.add),
                              reads=[a1, R], writes=[a1])
                    tap(k1 - 1)
                if si == 1 and pend_act is not None:
                    pend_mm = self.ln_stat_act(*pend_act)
                    pend_act = None
                if si == 3 and pend_mm is not None:
                    self.ln_stat_mm(*pend_mm)
                    pend_mm = None
                if si == 3:
                    acc = A[0]
                    tr.op("dve", lambda E, acc=acc, a1=A[1]: E.tensor_tensor(out=acc.ap, in0=acc.ap, in1=a1.ap, op=ALU.add),
                          reads=[acc, A[1]], writes=[acc])
                    pend_act = (j, acc)
                yield
                if si == 0 and j + 1 < KC:
                    nxt = head(j + 1)
                    yield
            ctx = nxt
        self._m1_pend = pend_act

    def pe_taps(self, ab, wa, k0):
        tr = self.tr
        V = self.vecs
        a16 = self.r_ab16.next()
        a16t = a16.ap
        tr.op("act", lambda E: E.activation(out=a16t[:, 0:HA + T], in_=ab.ap[:, 0:HA + T], func=AF.Copy),
              reads=[ab], writes=[a16])
        R = self.r_ps.next()
        ks = list(range(k0, CA))
        for q, k in enumerate(ks):
            dg = self.r_diag.next()
            tr.op("act", lambda E, dg=dg, k=k: E.activation(out=dg.ap, in_=V[:, V_ID:V_ID + 128], func=AF.Copy,
                                                       scale=V[:, wa + k:wa + k + 1]),
                  reads=[self.b_vecs], writes=[dg])
            tr.op("pe", lambda E, dg=dg, k=k, q=q: E.matmul(R.ap, lhsT=dg.ap, rhs=a16t[:, k:k + T],
                                                          start=(q == 0), stop=(q == len(ks) - 1)),
                  reads=[dg, a16], writes=([R] if q == 0 else []), mark=True)
        R.buf.w = ("pe", tr.cnt["pe"])
        return R

    def ln_stat_act(self, j, acc):
        tr = self.tr
        sq = self.r_sqL.next()
        tr.op("act", lambda E: E.activation(out=sq.ap, in_=acc.ap, func=AF.Square), reads=[acc], writes=[sq])
        ab16 = self.r_accb.next()
        tr.op("act", lambda E: E.activation(out=ab16.ap, in_=acc.ap, func=AF.Copy), reads=[acc], writes=[ab16])
        return (j, ab16, sq)

    def ln_m3_m4(self, tile, h, h_next, side=None):
        tr = self.tr
        u = self.b_uM
        V = self.vecs

        def g_groups(i):
            sbuf, st = self.slab(tile, "m3", 2 * i + 1)
            res = []
            for ci in (0, 1):
                ps = self.r_ps.next()
                self.mm_group(ps, [(st[:, (ci * 8 + k) * 128:(ci * 8 + k + 1) * 128], sbuf, u[k]) for k in range(KC)])
                th = self.r_th.next()
                tr.op("act", lambda E, th=th, ps=ps: E.activation(out=th.ap, in_=ps.ap, func=AF.Tanh, scale=0.5),
                      reads=[ps], writes=[th])
                res.append(th)
            return res
        if side is not None:
            for _ in range(HOLD_A):
                if next(side, "end") == "end":
                    break
        pend_mm = self.ln_stat_act(*self._m1_pend)
        self.ln_stat_mm(*pend_mm)
        ths = {0: g_groups(0)}
        mean = self.b_mean
        tr.op("act", lambda E: E.activation(out=mean.ap, in_=self.b_S0.ap, func=AF.Copy), reads=[self.b_S0], writes=[mean])
        m2 = self.r_f.next()
        tr.op("dve", lambda E: E.tensor_tensor(out=m2.ap, in0=mean.ap, in1=mean.ap, op=ALU.mult), reads=[mean], writes=[m2])
        var = self.r_f.next()
        tr.op("dve", lambda E: E.tensor_tensor(out=var.ap, in0=self.b_S1.ap, in1=m2.ap, op=ALU.subtract),
              reads=[self.b_S1, m2], writes=[var])
        var2 = self.r_f.next()
        tr.op("dve", lambda E: E.tensor_scalar_max(out=var2.ap, in0=var.ap, scalar1=0.0), reads=[var], writes=[var2])
        sd = self.r_f.next()
        tr.op("act", lambda E: E.activation(out=sd.ap, in_=var2.ap, func=AF.Sqrt, bias=EPS, scale=1.0), reads=[var2], writes=[sd])
        lnr = self.b_lnr
        tr.op("dve", lambda E: E.reciprocal(out=lnr.ap, in_=sd.ap), reads=[sd], writes=[lnr])
        nmr = self.b_nmr
        tr.op("dve", lambda E: E.scalar_tensor_tensor(out=nmr.ap, in0=mean.ap, scalar=-1.0, in1=lnr.ap,
                                                      op0=ALU.mult, op1=ALU.mult), reads=[mean, lnr], writes=[nmr])
        xns = []
        for j in range(KC + 1):
            if j < KC:
                xn = self.r_f.next()
                acc = self.b_acc[j]
                tr.op("dve", lambda E, xn=xn, acc=acc: E.tensor_tensor(out=xn.ap, in0=acc.ap, in1=lnr.ap, op=ALU.mult),
                      reads=[acc, lnr], writes=[xn])
                xns.append(xn)
            if j >= 1:
                jj = j - 1
                xn = xns[jj]
                xo = self.r_f.next()
                tr.op("dve", lambda E, xn=xn, xo=xo: E.tensor_tensor(out=xo.ap, in0=xn.ap, in1=nmr.ap, op=ALU.add),
                      reads=[xn, nmr], writes=[xo])
                tr.op("act", lambda E, xo=xo, jj=jj: E.activation(
                    out=self.b_aact[jj].ap, in_=xo.ap, func=AF.Silu, bias=V[:, V_ALNB + jj:V_ALNB + jj + 1],
                    scale=V[:, V_ALNG + jj:V_ALNG + jj + 1]), reads=[xo, self.b_vecs], writes=[self.b_aact[jj]])
        if side is not None:
            for _ in side:
                pass
        ths[1] = g_groups(1)
        prev = None
        for i in range(KC):
            sbuf, st = self.slab(tile, "m3", 2 * i)
            tha, thb = ths.pop(i)
            ps_ya = self.r_ps.next()
            self.mm_group(ps_ya, [(st[:, (0 * 8 + k) * 128:(0 * 8 + k + 1) * 128], sbuf, self.b_aact[k]) for k in range(KC)])
            ps_yb = self.r_ps.next()
            self.mm_group(ps_yb, [(st[:, (1 * 8 + k) * 128:(1 * 8 + k + 1) * 128], sbuf, self.b_bact[k]) for k in range(KC)])
            t1 = self.r_f.next()
            tr.op("dve", lambda E, t1=t1, tha=tha, ps=ps_ya: E.scalar_tensor_tensor(
                out=t1.ap, in0=tha.ap, scalar=1.0, in1=ps.ap, op0=ALU.add, op1=ALU.mult), reads=[tha, ps_ya], writes=[t1])
            t2 = self.r_f.next()
            tr.op("dve", lambda E, t2=t2, thb=thb, ps=ps_yb: E.scalar_tensor_tensor(
                out=t2.ap, in0=thb.ap, scalar=1.0, in1=ps.ap, op0=ALU.add, op1=ALU.mult), reads=[thb, ps_yb], writes=[t2])
            if prev is not None:
                p1, p2, pi = prev
                tr.op("dve", lambda E, p1=p1, p2=p2, pi=pi: E.tensor_tensor(out=self.b_mm[pi].ap, in0=p1.ap, in1=p2.ap, op=ALU.add),
                      reads=[p1, p2], writes=[self.b_mm[pi]])
            prev = (t1, t2, i)
            if i + 2 < KC:
                ths[i + 2] = g_groups(i + 2)
        p1, p2, pi = prev
        tr.op("dve", lambda E: E.tensor_tensor(out=self.b_mm[pi].ap, in0=p1.ap, in1=p2.ap, op=ALU.add),
              reads=[p1, p2], writes=[self.b_mm[pi]])
        if h_next is not None:
            self.mixer_norm(h_next)
        for i in range(KC):
            sbuf, st = self.slab(tile, "wo", i // 2)
            ci = i % 2
            ps = self.r_ps.next()
            self.mm_group(ps, [(st[:, (ci * 8 + k) * 128:(ci * 8 + k + 1) * 128], sbuf, self.b_mm[k]) for k in range(KC)])
            tr.op("dve", lambda E, i=i, ps=ps: E.scalar_tensor_tensor(
                out=h[i].ap, in0=ps.ap, scalar=0.5, in1=h[i].ap, op0=ALU.mult, op1=ALU.add),
                reads=[ps, h[i]], writes=[h[i]])
        self.ffn_norm(h, 2)

    def final_store(self, tile, h):
        tr = self.tr
        rstd = self.norm_stats(h, self.b_rstdM)
        for k in range(KC):
            tr.op("dve", lambda E, k=k: E.scalar_tensor_tensor(
                out=h[k].ap, in0=h[k].ap, scalar=self.vecs[:, V_FINAL + k:V_FINAL + k + 1],
                in1=rstd.ap, op0=ALU.mult, op1=ALU.mult), reads=[h[k], rstd, self.b_vecs], writes=[h[k]])
        p = tile % 2
        dst = self.outT.rearrange("(k p) t -> p k t", p=128)[:, :, tile * T:(tile + 1) * T]
        self.tr.dma("pool", "ost%d" % p, self.b_out, h[0], dst, self.hT[p][:, :, :], extra_reads=h[1:])


_CACHE = {}


def _get_nc():
    if "nc" not in _CACHE:
        _CACHE["nc"] = Builder().build()
    return _CACHE["nc"]


def kernel(**inputs):
    inp = {k: np.asarray(v) for k, v in inputs.items()}
    x = np.asarray(inp["x"], np.float32)
    wpack = pack_weights(inp)
    vecs = pack_vecs(inp)
    nc = Builder().build()
    in_maps = []
    for c in range(N_CORES):
        in_maps.append({"xT": np.ascontiguousarray(x[c].T), "wpack": wpack, "vecs": vecs})
    res = run_bass_kernel_spmd(nc, in_maps, core_ids=list(range(N_CORES)))
    out = np.empty((N_CORES, SEQ, D), np.float32)
    for c in range(N_CORES):
        out[c] = np.asarray(res.results[c]["outT"]).T
    return out
```

```python
import numpy as np
import concourse.bass as bass
import concourse.mybir as mybir
from concourse.bass_utils import run_bass_kernel_spmd

F32 = mybir.dt.float32
BF16 = mybir.dt.bfloat16
ALU = mybir.AluOpType
AF = mybir.ActivationFunctionType

N_CORES = 8
D = 1024
KC = 8
SEQ = 4096
T = 512
NT = SEQ // T
FF = 2816
FC = FF // 128
CA = 31
CB = 3
HA = CA - 1
HB = CB - 1
EPS = 1e-6
D_IN = 7 * D

V_FFN1 = 0
V_MIX = 8
V_FFN2 = 16
V_FINAL = 24
V_ADWB = 32
V_ALNG = 40
V_ALNB = 48
V_ADW = 56
V_BCW = V_ADW + 8 * CA
V_ID = V_BCW + 8 * CB
NV = V_ID + 128
NP_PE_FIRST = 8
NP_PE_LAST = 12

SLAB = 2048


def slab_plan():
    plan = []
    for which in (1, 2):
        for f in range(FC):
            plan.append(("gu%d" % which, f, 2048))
        for d in range(2 * KC):
            plan.append(("dn%d" % which, d, 11 * 128))
        if which == 1:
            for s in range(20):
                plan.append(("win", s, 2048))
            for i in range(16):
                plan.append(("m3", i, 2048))
            for s in range(4):
                plan.append(("wo", s, 2048))
    return plan


PLAN = slab_plan()
NSLAB = len(PLAN)
SLAB_OFF = np.concatenate([[0], np.cumsum([p[2] for p in PLAN])]).astype(np.int64)
WTOT = int(SLAB_OFF[-1])
M1_SEQ = []
for _j in range(8):
    M1_SEQ += [_j, 8 + _j, 24 + _j, 32 + _j, 16 + _j]


def _kchunks(w, c0, ncols=128):
    kc = w.shape[0] // 128
    return w[:, c0:c0 + ncols].reshape(kc, 128, ncols).transpose(1, 0, 2)


def pack_weights(inp):
    out = np.empty((128, WTOT), np.float32)
    for n, (kind, idx, ne) in enumerate(PLAN):
        o = int(SLAB_OFF[n])
        if kind.startswith("gu"):
            wg = (inp["ffn1_w_gate"] if kind[2] == "1" else inp["ffn2_w_gate"])[0]
            wu = (inp["ffn1_w_up"] if kind[2] == "1" else inp["ffn2_w_up"])[0]
            parts = [_kchunks(wg, idx * 128).reshape(128, -1), _kchunks(wu, idx * 128).reshape(128, -1)]
            out[:, o:o + ne] = np.concatenate(parts, axis=1)
        elif kind.startswith("dn"):
            wd = (inp["ffn1_w_down"] if kind[2] == "1" else inp["ffn2_w_down"])[0]
            d, half = idx // 2, idx % 2
            full = _kchunks(wd, d * 128)
            out[:, o:o + ne] = full[:, half * 11:(half + 1) * 11, :].reshape(128, -1)
        elif kind == "win":
            w = inp["w_in"][0]
            parts = [_kchunks(w, M1_SEQ[2 * idx + ci] * 128).reshape(128, -1) for ci in range(2)]
            out[:, o:o + ne] = np.concatenate(parts, axis=1)
        elif kind == "m3":
            i, g = idx // 2, idx % 2
            if g == 0:
                parts = [_kchunks(inp["a_w_out"][0], i * 128).reshape(128, -1),
                         _kchunks(inp["b_w_out"][0], i * 128).reshape(128, -1)]
            else:
                parts = [_kchunks(inp["w_in"][0], (40 + i) * 128).reshape(128, -1),
                         _kchunks(inp["w_in"][0], (48 + i) * 128).reshape(128, -1)]
            out[:, o:o + ne] = np.concatenate(parts, axis=1)
        elif kind == "wo":
            w = inp["w_o"][0]
            parts = [_kchunks(w, (2 * idx + ci) * 128).reshape(128, -1) for ci in range(2)]
            out[:, o:o + ne] = np.concatenate(parts, axis=1)
        else:
            raise AssertionError(kind)
    return out


def pack_vecs(inp):
    v = np.zeros((128, NV), np.float32)

    def col(a):
        return np.asarray(a, np.float32).reshape(8, 128).T

    v[:, V_FFN1:V_FFN1 + 8] = col(inp["ffn1_norm"][0])
    v[:, V_MIX:V_MIX + 8] = col(inp["mix_norm"][0])
    v[:, V_FFN2:V_FFN2 + 8] = col(inp["ffn2_norm"][0])
    v[:, V_FINAL:V_FINAL + 8] = col(inp["final_norm"])
    v[:, V_ADWB:V_ADWB + 8] = col(inp["a_dw_b"][0])
    v[:, V_ALNG:V_ALNG + 8] = col(inp["a_ln_g"][0])
    v[:, V_ALNB:V_ALNB + 8] = col(inp["a_ln_b"][0])
    adw = np.asarray(inp["a_dw_w"][0], np.float32)
    v[:, V_ADW:V_ADW + 8 * CA] = adw.reshape(CA, 8, 128).transpose(2, 1, 0).reshape(128, 8 * CA)
    bcw = np.asarray(inp["b_conv_w"][0], np.float32)
    v[:, V_BCW:V_BCW + 8 * CB] = bcw.reshape(CB, 8, 128).transpose(2, 1, 0).reshape(128, 8 * CB)
    v[:, V_ID:V_ID + 128] = np.eye(128, dtype=np.float32)
    return v


class Buf:
    __slots__ = ("ap", "w", "r", "name", "gen")

    def __init__(self, ap, name=""):
        self.ap = ap
        self.w = None
        self.r = []
        self.name = name
        self.gen = 0


class View:
    __slots__ = ("ap", "bufs")

    def __init__(self, ap, bufs):
        self.ap = ap
        self.bufs = bufs


def vb(x):
    return list(x.bufs) if isinstance(x, View) else [x]


class Tracker:
    def __init__(self, nc, engines, sems):
        self.nc = nc
        self.eng = engines
        self.sem = sems
        self.cnt = {k: 0 for k in sems}
        self.waited = {e: {} for e in engines}
        self.n_wait = 0

    def _waits(self, eng, reads, writes):
        toks = {}
        for b in reads:
            if b.w is not None:
                toks[b.w[0]] = max(toks.get(b.w[0], 0), b.w[1])
        for b in writes:
            if b.w is not None:
                toks[b.w[0]] = max(toks.get(b.w[0], 0), b.w[1])
            for t in b.r:
                toks[t[0]] = max(toks.get(t[0], 0), t[1])
        E = self.eng[eng]
        wd = self.waited[eng]
        for sk in sorted(toks):
            val = toks[sk]
            if sk == "pe" and eng == "pe":
                continue
            if wd.get(sk, 0) < val:
                E.wait_ge(self.sem[sk], val)
                wd[sk] = val
                self.n_wait += 1

    def _record(self, tok, reads, writes):
        for b in reads:
            b.r.append(tok)
        for b in writes:
            b.w = tok
            b.r = []

    def op(self, eng, fn, reads=(), writes=(), mark=True):
        reads, writes = _unref(reads), _unref(writes)
        self._waits(eng, reads, writes)
        ins = fn(self.eng[eng])
        if mark:
            self.cnt[eng] += 1
            ins.then_inc(self.sem[eng], 1)
            tok = (eng, self.cnt[eng])
        else:
            tok = (eng, self.cnt[eng] + 1)
        self._record(tok, reads, writes)
        return tok

    def dma(self, queue, semkey, out_buf, in_buf, out_ap, in_ap, extra_reads=(), extra_writes=()):
        reads = _unref([in_buf] + list(extra_reads))
        writes = _unref([out_buf] + list(extra_writes))
        self._waits(queue, reads, writes)
        self.eng[queue].dma_start(out=out_ap, in_=in_ap).then_inc(self.sem[semkey], 16)
        self.cnt[semkey] += 16
        tok = (semkey, self.cnt[semkey])
        self._record(tok, reads, writes)
        return tok

    def wait_tok(self, eng, tok):
        if self.waited[eng].get(tok[0], 0) < tok[1]:
            self.eng[eng].wait_ge(self.sem[tok[0]], tok[1])
            self.waited[eng][tok[0]] = tok[1]


class NullTracker:
    def __init__(self):
        import collections
        self.cnt = collections.defaultdict(int)

    def op(self, eng, fn, reads=(), writes=(), mark=True):
        _unref(reads), _unref(writes)
        return None

    def dma(self, *a, **k):
        return None

    def wait_tok(self, *a, **k):
        return None


class Ref:
    __slots__ = ("buf", "gen", "ap")

    def __init__(self, buf, gen):
        self.buf = buf
        self.gen = gen
        self.ap = buf.ap


def _unref(items):
    out = []
    for b in items:
        if isinstance(b, Ref):
            assert b.gen == b.buf.gen, "stale ring buffer use: " + b.buf.name
            b = b.buf
        out.append(b)
    return out


class Ring:
    def __init__(self, bufs, name="ring"):
        self.bufs = bufs
        self.i = 0
        for q, b in enumerate(bufs):
            b.name = "%s%d" % (name, q)

    def next(self):
        b = self.bufs[self.i % len(self.bufs)]
        self.i += 1
        b.gen += 1
        return Ref(b, b.gen)


NSLOT = 10
NCAST = 4
HOLD_A = 4
DIRECT_CAST = True


class Builder:
    def __init__(self, nt=NT, debug_stage=None):
        self.nt = nt
        self.debug_stage = debug_stage
        nc = bass.Bass("TRN2", target_bir_lowering=False)
        self.nc = nc
        self.xT = nc.dram_tensor("xT", [D, SEQ], F32, kind="ExternalInput").ap()
        self.wpack = nc.dram_tensor("wpack", [128, WTOT], F32, kind="ExternalInput").ap()
        self.vecs_d = nc.dram_tensor("vecs", [128, NV], F32, kind="ExternalInput").ap()
        self.outT = nc.dram_tensor("outT", [D, SEQ], F32, kind="ExternalOutput").ap()
        self.wbf = nc.dram_tensor("wbf", [128, WTOT], BF16).ap()

    def sb(self, name, shape, dt):
        t = self.stack.enter_context(self.nc.sbuf_tensor(name, shape, dt))
        return t

    def build(self):
        from contextlib import ExitStack
        nc = self.nc
        with ExitStack() as st:
            self.stack = st
            sb = self.sb
            self.hT = [sb("hT%d" % i, [128, KC, T], F32) for i in range(2)]
            self.uF = sb("uF", [128, KC, T], BF16)
            self.uM = sb("uM", [128, KC, T], BF16)
            self.act = sb("act", [128, FC, T], BF16)
            self.acc = sb("acc", [128, KC, T], F32)
            self.acc1 = [sb("acc1_%d" % i, [128, T], F32) for i in range(2)]
            self.bact = sb("bact", [128, KC, T], BF16)
            self.aact = sb("aact", [128, KC, T], BF16)
            self.mm = sb("mm", [128, KC, T], BF16)
            self.abuf = [sb("abuf%d" % i, [128, HA + T], F32) for i in range(3)]
            self.ab16 = [sb("ab16_%d" % i, [128, HA + T + 2], BF16) for i in range(1)]
            self.diag = [sb("diag%d" % i, [128, 128], BF16) for i in range(4)]
            self.cxb = [sb("cx%d" % i, [128, HB + T], F32) for i in range(2)]
            self.ahist = sb("ahist", [128, KC, HA], F32)
            self.bhist = sb("bhist", [128, KC, HB], F32)
            self.sq = [sb("sq%d" % i, [128, T], BF16) for i in range(2)]
            self.accb = [sb("accb%d" % i, [128, T], BF16) for i in range(2)]
            self.sqL = [sb("sqL%d" % i, [128, T], BF16) for i in range(2)]
            self.fpool = [sb("fp%d" % i, [128, T], F32) for i in range(7)]
            self.thb = [sb("thg%d" % i, [128, T], F32) for i in range(4)]
            self.rstdF = sb("rstdF", [128, T], F32)
            self.rstdM = sb("rstdM", [128, T], F32)
            self.meanb = sb("meanb", [128, T], F32)
            self.lnr = sb("lnr", [128, T], F32)
            self.nmr = sb("nmr", [128, T], F32)
            self.vecs = sb("vecs_sb", [128, NV], F32)
            self.ones = sb("ones", [128, 128], BF16)
            self.slots = [sb("wslot%d" % i, [128, SLAB], BF16) for i in range(NSLOT)]
            self.psum = [st.enter_context(nc.psum_tensor("ps%d" % i, [128, T], F32)) for i in range(8)]
            semnames = ["pe", "act", "dve", "pool"] + ["slot%d" % i for i in range(NSLOT)] + \
                       ["cslot%d" % i for i in range(NSLOT)] + \
                       ["cast%d" % i for i in range(NCAST)] + ["xld0", "xld1", "xld2", "ost0", "ost1", "misc"]
            sems = {k: st.enter_context(nc.semaphore(k)) for k in semnames}
            block = st.enter_context(nc.Block())
            self.block = block
            engs = {}
            self._emit_all(block, sems)
        return nc

    def _emit_all(self, block, sems):
        nc = self.nc
        rec = {"pe": [], "act": [], "dve": [], "pool": [], "sp": []}

        class Proxy:
            def __init__(self, name):
                self.name = name

            def __getattr__(self, meth):
                lst = rec[self.name]

                def call(*a, **kw):
                    entry = [meth, a, kw, []]
                    lst.append(entry)

                    class H:
                        def then_inc(_s, sem, val=1):
                            entry[3].append((sem, val))
                            return _s
                    return H()
                return call

        engines = {k: Proxy(k) for k in rec}
        self.tr = Tracker(nc, engines, sems)
        self._program()

        def replay(name):
            def body(E):
                for meth, a, kw, incs in rec[name]:
                    ins = getattr(E, meth)(*a, **kw)
                    for sem, val in incs:
                        ins.then_inc(sem, val)
            return body

        block.tensor(replay("pe"))
        block.scalar(replay("act"))
        block.vector(replay("dve"))
        block.gpsimd(replay("pool"))
        block.sync(replay("sp"))

    def _mkbufs(self):
        B = Buf
        self.b_h = [[B(self.hT[p][:, k, :], "h%d_%d" % (p, k)) for k in range(KC)] for p in range(2)]
        self.b_uF = [B(self.uF[:, k, :]) for k in range(KC)]
        self.b_uM = [B(self.uM[:, k, :]) for k in range(KC)]
        self.b_act = [B(self.act[:, f, :]) for f in range(FC)]
        self.b_acc = [B(self.acc[:, k, :]) for k in range(KC)]
        self.b_acc1 = [B(t[:]) for t in self.acc1]
        self.b_bact = [B(self.bact[:, k, :]) for k in range(KC)]
        self.b_aact = [B(self.aact[:, k, :]) for k in range(KC)]
        self.b_mm = [B(self.mm[:, k, :]) for k in range(KC)]
        self.r_abuf = Ring([B(t) for t in self.abuf], "abuf")
        self.r_ab16 = Ring([B(t) for t in self.ab16], "ab16")
        self.r_diag = Ring([B(t[:]) for t in self.diag], "diag")
        self.r_cx = Ring([B(t) for t in self.cxb], "cx")
        self.b_ahist = [B(self.ahist[:, k, :]) for k in range(KC)]
        self.b_bhist = [B(self.bhist[:, k, :]) for k in range(KC)]
        self.r_sq = Ring([B(t[:]) for t in self.sq], "sq")
        self.r_accb = Ring([B(t[:]) for t in self.accb], "accb")
        self.r_sqL = Ring([B(t[:]) for t in self.sqL], "sqL")
        self.r_f = Ring([B(t[:]) for t in self.fpool], "fpool")
        self.r_th = Ring([B(t[:]) for t in self.thb], "thg")
        self.b_rstdF = B(self.rstdF[:])
        self.b_rstdM = B(self.rstdM[:])
        self.b_mean = B(self.meanb[:])
        self.b_lnr = B(self.lnr[:])
        self.b_nmr = B(self.nmr[:])
        self.b_vecs = B(self.vecs[:])
        self.b_ones = B(self.ones[:])
        self.b_slot = [B(t) for t in self.slots]
        self.r_ps = Ring([B(self.psum[i][:]) for i in range(6)], "psum")
        self.b_S0 = B(self.psum[6][:])
        self.b_S1 = B(self.psum[7][:])
        self.b_wsrc = B(None, "wpack")
        self.b_scr = {(p[0], p[1]): B(None, "scr") for p in PLAN}
        self.b_xsrc = B(None, "xT")
        self.b_out = B(None, "outT")

    def _program(self):
        real_tr = self.tr
        self.tr = NullTracker()
        self.dry = True
        self.reqs = []
        self._mkbufs()
        self._body()
        reqs = self.reqs
        self.loads = []
        self.last_req = {}
        seen = set()
        for r, key in enumerate(reqs):
            if key not in seen:
                seen.add(key)
                self.loads.append(key)
            self.last_req[key] = r
        self.tr = real_tr
        self.dry = False
        self.req_i = 0
        self.next_load = 0
        self.slot_key = [None] * NSLOT
        self.slot_of = {}
        self.free_q = list(range(NSLOT))
        self._mkbufs()
        self._setup()
        self._body()
        tr = self.tr
        for p in range(2):
            k = "ost%d" % p
            if tr.cnt[k]:
                tr.wait_tok("sp", (k, tr.cnt[k]))

    def _setup(self):
        tr = self.tr
        B = Buf
        tr.dma("sp", "misc", self.b_vecs, B(None), self.vecs[:], self.vecs_d[:])
        tr.op("dve", lambda E: E.memset(self.ones[:], 1.0 / D), writes=[self.b_ones])
        tr.op("dve", lambda E: E.memset(self.ahist[:], 0.0), writes=self.b_ahist)
        tr.op("dve", lambda E: E.memset(self.bhist[:], 0.0), writes=self.b_bhist)
        tr.op("dve", lambda E: E.tensor_scalar_mul(out=self.vecs[:, V_ADW:V_ADW + 8 * CA],
                                                   in0=self.vecs[:, V_ADW:V_ADW + 8 * CA], scalar1=0.5),
              reads=[self.b_vecs], writes=[self.b_vecs])
        if self.nt > 1:
            self.load_x_staged(1)
        self.pidx = {(p[0], p[1]): n for n, p in enumerate(PLAN)}
        self.cast_seen = set()
        self.slot_dirty = [None] * NSLOT
        self.nwb = 0
        if DIRECT_CAST:
            return
        order = []
        seen = set()
        for key in self.loads:
            k2 = (key[1], key[2])
            if k2 not in seen:
                seen.add(k2)
                order.append(k2)
        assert len(order) == NSLAB
        pidx = {(p[0], p[1]): n for n, p in enumerate(PLAN)}
        for q, k2 in enumerate(order):
            n = pidx[k2]
            o, ne = int(SLAB_OFF[n]), PLAN[n][2]
            sk = "cast%d" % (q % NCAST)
            if q >= NCAST:
                tr.wait_tok("pool", (sk, 16 * (q // NCAST)))
            tr.dma("pool", sk, self.b_scr[k2], self.b_wsrc, self.wbf[:, o:o + ne], self.wpack[:, o:o + ne])
        self.pidx = pidx

    def _body(self):
        nt = self.nt
        self.load_x(0)
        for _ in self.gen_ffn(0, self.b_h[0], 1):
            pass
        self.mixer_norm(self.b_h[0])
        HOLD = 8
        for i in range(nt):
            h = self.b_h[i % 2]
            if i >= 1 and i + 1 < nt:
                self.load_x_staged(i + 1)
            if i == 0:
                nb = 31 if nt > 1 else 0
            elif i + 1 < nt:
                nb = 62
            else:
                nb = 31
            gb = self.gen_side(i) if nb else None
            self.merge(self.gen_m1(i, h), 40, gb, max(nb - HOLD, 0), delay_b=1, stop_b=max(nb - HOLD, 0))
            self.ln_m3_m4(i, h, (self.b_h[(i + 1) % 2] if i + 1 < nt else None), gb)
        hl = self.b_h[(nt - 1) % 2]
        for _ in self.gen_ffn_f(nt - 1, 2):
            pass
        for _ in self.gen_ffn_d(nt - 1, hl, 2, hl):
            pass
        self.final_store(nt - 1, hl)

    def gen_side(self, i):
        nt = self.nt
        views, _, _ = self.x_staging()
        hn = self.b_h[(i + 1) % 2]
        if i == 0:
            self.ffn_norm(views, 1)
            yield
            yield from self.gen_ffn_f(1, 1)
            yield from self.gen_ffn_d(1, hn, 1, views)
            return
        hp = self.b_h[(i - 1) % 2]
        yield from self.gen_ffn_f(i - 1, 2)
        if i + 1 < nt:
            self.ffn_norm(views, 1)
            yield
        yield from self.gen_ffn_d(i - 1, hp, 2, hp)
        if i + 1 < nt:
            gf = self.gen_ffn_f(i + 1, 1)
            next(gf)
            yield
            next(gf)
            yield
            self.final_store(i - 1, hp)
            yield
            yield from gf
            yield from self.gen_ffn_d(i + 1, hn, 1, views)
        else:
            self.final_store(i - 1, hp)
            yield

    @staticmethod
    def merge(ga, na, gb, nb, delay_b=0, stop_b=None):
        da = db = 0
        a_live, b_live = True, (gb is not None and (stop_b is None or stop_b > 0))
        while a_live or b_live:
            pick_a = a_live and (not b_live or da < delay_b or da * nb <= db * na)
            if pick_a:
                try:
                    next(ga)
                    da += 1
                except StopIteration:
                    a_live = False
            else:
                try:
                    next(gb)
                    db += 1
                    if stop_b is not None and db >= stop_b:
                        b_live = False
                except StopIteration:
                    b_live = False

    def load_x(self, i):
        p = i % 2
        src = self.xT.rearrange("(k p) t -> p k t", p=128)[:, :, i * T:(i + 1) * T]
        self.tr.dma("sp", "xld2", self.b_h[p][0], self.b_xsrc, self.hT[p][:, :, :], src,
                    extra_writes=self.b_h[p][1:])

    def x_staging(self):
        a = self.aact[:].rearrange("p k t -> p (k t)").bitcast(F32)
        m = self.mm[:].rearrange("p k t -> p (k t)").bitcast(F32)
        views = []
        for k in range(KC):
            t, bufs = (a, self.b_aact) if k < 4 else (m, self.b_mm)
            kk = k % 4
            views.append(View(t[:, kk * T:(kk + 1) * T], [bufs[2 * kk], bufs[2 * kk + 1]]))
        return views, a, m

    def load_x_staged(self, i):
        _, a, m = self.x_staging()
        xs = self.xT.rearrange("(k p) t -> p k t", p=128)
        self.tr.dma("pool", "xld0", self.b_aact[0], self.b_xsrc, a.rearrange("p (k t) -> p k t", k=4),
                    xs[:, 0:4, i * T:(i + 1) * T], extra_writes=self.b_aact[1:])
        self.tr.dma("pool", "xld1", self.b_mm[0], self.b_xsrc, m.rearrange("p (k t) -> p k t", k=4),
                    xs[:, 4:8, i * T:(i + 1) * T], extra_writes=self.b_mm[1:])

    def store_out(self, i):
        p = i % 2
        dst = self.outT.rearrange("(k p) t -> p k t", p=128)[:, :, i * T:(i + 1) * T]
        self.tr.dma("sp", "ost%d" % p, self.b_out, self.b_h[p][0], dst, self.hT[p][:, :, :],
                    extra_reads=self.b_h[p][1:])

    def slab(self, tile, kind, idx):
        key = (tile, kind, idx)
        if self.dry:
            if not self.reqs or self.reqs[-1] != key:
                self.reqs.append(key)
            return self.b_slot[0], self.slots[0]
        tr = self.tr
        if self.req_i > 0 and self.reqs[self.req_i - 1] == key:
            s = self.slot_of[key]
            return self.b_slot[s], self.slots[s]
        r = self.req_i
        assert self.reqs[r] == key, (self.reqs[r], key)
        self.req_i += 1
        for s in range(NSLOT):
            k2 = self.slot_key[s]
            if k2 is not None and self.last_req[k2] < r - 1:
                self.slot_key[s] = None
                del self.slot_of[k2]
                self.free_q.append(s)
                if self.slot_dirty[s] is not None:
                    self._writeback(s)
        while self.next_load < len(self.loads) and self.free_q:
            s = self.free_q.pop(0)
            k2 = self.loads[self.next_load]
            kk = (k2[1], k2[2])
            n = self.pidx[kk]
            o, ne = int(SLAB_OFF[n]), PLAN[n][2]
            if DIRECT_CAST and kk not in self.cast_seen:
                self.cast_seen.add(kk)
                tr.dma("pool", "cslot%d" % s, self.b_slot[s], self.b_wsrc,
                       self.slots[s][:, 0:ne], self.wpack[:, o:o + ne])
                self.slot_dirty[s] = kk
            else:
                if kk in self.slot_dirty:
                    self._writeback(self.slot_dirty.index(kk))
                tr.dma("sp", "slot%d" % s, self.b_slot[s], self.b_scr[kk],
                       self.slots[s][:, 0:ne], self.wbf[:, o:o + ne])
            self.slot_key[s] = k2
            self.slot_of[k2] = s
            self.next_load += 1
        assert key in self.slot_of, ("weight slab not resident", key)
        s = self.slot_of[key]
        return self.b_slot[s], self.slots[s]

    def _writeback(self, s):
        tr = self.tr
        kk = self.slot_dirty[s]
        n = self.pidx[kk]
        o, ne = int(SLAB_OFF[n]), PLAN[n][2]
        q = self.nwb
        sk = "cast%d" % (q % NCAST)
        if q >= NCAST:
            tr.wait_tok("sp", (sk, 16 * (q // NCAST)))
        tr.dma("sp", sk, self.b_scr[kk], self.b_slot[s], self.wbf[:, o:o + ne], self.slots[s][:, 0:ne])
        self.nwb += 1
        self.slot_dirty[s] = None

    def mm_group(self, ps, pairs):
        tr = self.tr
        n = len(pairs)
        for q, (lap, lbuf, rbuf) in enumerate(pairs):
            tr.op("pe", lambda E, lap=lap, rbuf=rbuf, q=q: E.matmul(
                ps.ap, lhsT=lap, rhs=rbuf.ap, start=(q == 0), stop=(q == n - 1)),
                reads=[lbuf, rbuf], writes=([ps] if q == 0 else []), mark=(q == n - 1))
        (ps.buf if isinstance(ps, Ref) else ps).w = ("pe", tr.cnt["pe"])

    def norm_stats(self, h, rstd):
        tr = self.tr
        ps = self.r_ps.next()
        for k in range(KC):
            sq = self.r_sq.next()
            tr.op("act", lambda E, k=k, sq=sq: E.activation(out=sq.ap, in_=h[k].ap, func=AF.Square),
                  reads=vb(h[k]), writes=[sq])
            tr.op("pe", lambda E, k=k, sq=sq: E.matmul(ps.ap, lhsT=self.ones[:], rhs=sq.ap,
                                                        start=(k == 0), stop=(k == KC - 1)),
                  reads=[self.b_ones, sq], writes=([ps] if k == 0 else []), mark=True)
        ps.buf.w = ("pe", tr.cnt["pe"])
        sd = self.r_f.next()
        tr.op("act", lambda E: E.activation(out=sd.ap, in_=ps.ap, func=AF.Sqrt, bias=EPS, scale=1.0),
              reads=[ps], writes=[sd])
        tr.op("dve", lambda E: E.reciprocal(out=rstd.ap, in_=sd.ap), reads=[sd], writes=[rstd])
        return rstd

    def normalize(self, h, u, vcol, rstd):
        tr = self.tr
        for k in range(KC):
            tr.op("dve", lambda E, k=k: E.scalar_tensor_tensor(
                out=u[k].ap, in0=h[k].ap, scalar=self.vecs[:, vcol + k:vcol + k + 1], in1=rstd.ap,
                op0=ALU.mult, op1=ALU.mult), reads=vb(h[k]) + [rstd, self.b_vecs], writes=[u[k]])

    def mixer_norm(self, h):
        rstd = self.norm_stats(h, self.b_rstdM)
        self.normalize(h, self.b_uM, V_MIX, rstd)

    def ffn_norm(self, src, which):
        vcol = V_FFN1 if which == 1 else V_FFN2
        rstd = self.norm_stats(src, self.b_rstdF)
        self.normalize(src, self.b_uF, vcol, rstd)

    def gen_ffn_f(self, tile, which):
        tr = self.tr
        u = self.b_uF
        gk = "gu%d" % which
        for f in range(FC):
            sbuf, st = self.slab(tile, gk, f)
            psg = self.r_ps.next()
            self.mm_group(psg, [(st[:, k * 128:(k + 1) * 128], sbuf, u[k]) for k in range(KC)])
            psu = self.r_ps.next()
            self.mm_group(psu, [(st[:, (8 + k) * 128:(8 + k + 1) * 128], sbuf, u[k]) for k in range(KC)])
            sg = self.r_f.next()
            tr.op("act", lambda E, sg=sg, psg=psg: E.activation(out=sg.ap, in_=psg.ap, func=AF.Silu),
                  reads=[psg], writes=[sg])
            a = self.b_act[f]
            tr.op("dve", lambda E, sg=sg, psu=psu, a=a: E.tensor_tensor(out=a.ap, in0=sg.ap, in1=psu.ap, op=ALU.mult),
                  reads=[sg, psu], writes=[a])
            yield

    def gen_ffn_d(self, tile, h, which, src):
        tr = self.tr
        dk = "dn%d" % which
        for d in range(KC):
            sb0, st0 = self.slab(tile, dk, 2 * d)
            sb1, st1 = self.slab(tile, dk, 2 * d + 1)
            psd = self.r_ps.next()
            self.mm_group(psd, [((st0 if f < 11 else st1)[:, (f % 11) * 128:(f % 11 + 1) * 128],
                                 (sb0 if f < 11 else sb1), self.b_act[f]) for f in range(FC)])
            tr.op("dve", lambda E, d=d, psd=psd: E.scalar_tensor_tensor(
                out=h[d].ap, in0=psd.ap, scalar=0.5, in1=src[d].ap, op0=ALU.mult, op1=ALU.add),
                reads=[psd] + vb(src[d]), writes=[h[d]])
            yield

    def gen_ffn(self, tile, h, which, xin=None):
        src = xin if xin is not None else h
        self.ffn_norm(src, which)
        yield
        yield from self.gen_ffn_f(tile, which)
        yield from self.gen_ffn_d(tile, h, which, src)

    def win_group(self, tile, c, u):
        sbuf, st = self.slab(tile, "win", c // 2)
        ci = c % 2
        ps = self.r_ps.next()
        self.mm_group(ps, [(st[:, (ci * 8 + k) * 128:(ci * 8 + k + 1) * 128], sbuf, u[k]) for k in range(KC)])
        return ps

    def ln_stat_mm(self, j, ab16, sq):
        tr = self.tr
        tr.op("pe", lambda E: E.matmul(self.b_S0.ap, lhsT=self.ones[:], rhs=ab16.ap,
                                       start=(j == 0), stop=(j == KC - 1)),
              reads=[self.b_ones, ab16], writes=([self.b_S0] if j == 0 else []), mark=False)
        tr.op("pe", lambda E: E.matmul(self.b_S1.ap, lhsT=self.ones[:], rhs=sq.ap,
                                       start=(j == 0), stop=(j == KC - 1)),
              reads=[self.b_ones, sq], writes=([self.b_S1] if j == 0 else []), mark=True)
        self.b_S0.w = ("pe", tr.cnt["pe"])
        self.b_S1.w = ("pe", tr.cnt["pe"])

    def gen_m1(self, tile, h):
        tr = self.tr
        u = self.b_uM
        V = self.vecs

        def head(j):
                ps_av = self.win_group(tile, 5 * j + 0, u)
                ps_ag = self.win_group(tile, 5 * j + 1, u)
                th = self.r_f.next()
                tr.op("act", lambda E, th=th, ps=ps_ag: E.activation(out=th.ap, in_=ps.ap, func=AF.Tanh, scale=0.5),
                      reads=[ps_ag], writes=[th])
                ab = self.r_abuf.next()
                abt = ab.ap
                tr.op("act", lambda E, abt=abt, j=j: E.activation(out=abt[:, 0:HA], in_=self.ahist[:, j, :], func=AF.Copy),
                      reads=[self.b_ahist[j]], writes=[ab])
                tr.op("dve", lambda E, abt=abt, th=th, ps=ps_av: E.scalar_tensor_tensor(
                    out=abt[:, HA:HA + T], in0=th.ap, scalar=1.0, in1=ps.ap, op0=ALU.add, op1=ALU.mult),
                    reads=[th, ps_av], writes=[ab])
                ps_bc = self.win_group(tile, 5 * j + 2, u)
                ps_bx = self.win_group(tile, 5 * j + 3, u)
                ps_bb = self.win_group(tile, 5 * j + 4, u)
                bx = self.r_f.next()
                tr.op("act", lambda E, bx=bx, ps=ps_bx: E.activation(out=bx.ap, in_=ps.ap, func=AF.Copy),
                      reads=[ps_bx], writes=[bx])
                cx = self.r_cx.next()
                cxt = cx.ap
                tr.op("act", lambda E, cxt=cxt, j=j: E.activation(out=cxt[:, 0:HB], in_=self.bhist[:, j, :], func=AF.Copy),
                      reads=[self.b_bhist[j]], writes=[cx])
                tr.op("dve", lambda E, cxt=cxt, bx=bx, ps=ps_bc: E.tensor_tensor(
                    out=cxt[:, HB:HB + T], in0=bx.ap, in1=ps.ap, op=ALU.mult), reads=[bx, ps_bc], writes=[cx])
                A = [self.b_acc[j], self.b_acc1[j % 2]]
                wa = V_ADW + j * CA
                wc = V_BCW + j * CB
                v = self.r_f.next()

                def tap(k):
                    a = A[k % 2]
                    if k == 0:
                        tr.op("dve", lambda E: E.tensor_scalar(
                            out=a.ap, in0=abt[:, 0:T], scalar1=V[:, wa:wa + 1], scalar2=V[:, V_ADWB + j:V_ADWB + j + 1],
                            op0=ALU.mult, op1=ALU.add), reads=[ab, self.b_vecs], writes=[a])
                    elif k == 1:
                        tr.op("dve", lambda E: E.tensor_scalar_mul(out=a.ap, in0=abt[:, 1:1 + T], scalar1=V[:, wa + 1:wa + 2]),
                              reads=[ab, self.b_vecs], writes=[a])
                    else:
                        tr.op("dve", lambda E: E.scalar_tensor_tensor(
                            out=a.ap, in0=abt[:, k:k + T], scalar=V[:, wa + k:wa + k + 1], in1=a.ap, op0=ALU.mult, op1=ALU.add),
                            reads=[ab, a, self.b_vecs], writes=[a])

                def btap(k):
                    if k == 0:
                        tr.op("dve", lambda E: E.tensor_scalar_mul(out=v.ap, in0=cxt[:, 0:T], scalar1=V[:, wc:wc + 1]),
                              reads=[cx, self.b_vecs], writes=[v])
                    else:
                        tr.op("dve", lambda E: E.scalar_tensor_tensor(
                            out=v.ap, in0=cxt[:, k:k + T], scalar=V[:, wc + k:wc + k + 1], in1=v.ap, op0=ALU.mult, op1=ALU.add),
                            reads=[cx, v, self.b_vecs], writes=[v])
                tap(0)
                btap(0)
                tap(1)
                btap(1)
                tr.op("dve", lambda E, abt=abt, j=j: E.tensor_copy(out=self.ahist[:, j, :], in_=abt[:, T:T + HA]),
                      reads=[ab], writes=[self.b_ahist[j]])
                tap(2)
                btap(2)
                tr.op("dve", lambda E, cxt=cxt, j=j: E.tensor_copy(out=self.bhist[:, j, :], in_=cxt[:, T:T + HB]),
                      reads=[cx], writes=[self.b_bhist[j]])
                tap(3)
                tr.op("dve", lambda E, v=v, ps=ps_bb, j=j: E.tensor_tensor(out=self.b_bact[j].ap, in0=v.ap, in1=ps.ap, op=ALU.mult),
                      reads=[v, ps_bb], writes=[self.b_bact[j]])

                return (j, tap, A, ab, wa)

        pend_act = None
        pend_mm = None
        ctx = head(0)
        yield
        np_pe = NP_PE_FIRST if tile == 0 else (NP_PE_LAST if tile == self.nt - 1 else 0)
        K_DVE = CA - np_pe
        nd = K_DVE - 4
        segs = [(4, 4 + nd // 4), (4 + nd // 4, 4 + nd // 2), (4 + nd // 2, 4 + (3 * nd) // 4), (4 + (3 * nd) // 4, K_DVE)]
        for j in range(KC):
            _, tap, A, ab, wa = ctx
            nxt = None
            R = None
            for si, (k0, k1) in enumerate(segs):
                for k in range(k0, k1 - (1 if si == 3 else 0)):
                    tap(k)
                if si == 2 and np_pe:
                    R = self.pe_taps(ab, wa, K_DVE)
                if si == 3:
                    if np_pe:
                        a1 = A[1]
                        tr.op("dve", lambda E, a1=a1, R=R: E.tensor_tensor(out=a1.ap, in0=a1.ap, in1=R.ap, op=ALU.add),
                              reads=[a1, R], writes=[a1])
                    tap(k1 - 1)
                if si == 1 and pend_act is not None:
                    pend_mm = self.ln_stat_act(*pend_act)
                    pend_act = None
                if si == 3 and pend_mm is not None:
                    self.ln_stat_mm(*pend_mm)
                    pend_mm = None
                if si == 3:
                    acc = A[0]
                    tr.op("dve", lambda E, acc=acc, a1=A[1]: E.tensor_tensor(out=acc.ap, in0=acc.ap, in1=a1.ap, op=ALU.add),
                          reads=[acc, A[1]], writes=[acc])
                    pend_act = (j, acc)
                yield
                if si == 0 and j + 1 < KC:
                    nxt = head(j + 1)
                    yield
            ctx = nxt
        self._m1_pend = pend_act

    def pe_taps(self, ab, wa, k0):
        tr = self.tr
        V = self.vecs
        a16 = self.r_ab16.next()
        a16t = a16.ap
        tr.op("act", lambda E: E.activation(out=a16t[:, 0:HA + T], in_=ab.ap[:, 0:HA + T], func=AF.Copy),
              reads=[ab], writes=[a16])
        R = self.r_ps.next()
        ks = list(range(k0, CA))
        for q, k in enumerate(ks):
            dg = self.r_diag.next()
            tr.op("act", lambda E, dg=dg, k=k: E.activation(out=dg.ap, in_=V[:, V_ID:V_ID + 128], func=AF.Copy,
                                                       scale=V[:, wa + k:wa + k + 1]),
                  reads=[self.b_vecs], writes=[dg])
            tr.op("pe", lambda E, dg=dg, k=k, q=q: E.matmul(R.ap, lhsT=dg.ap, rhs=a16t[:, k:k + T],
                                                          start=(q == 0), stop=(q == len(ks) - 1)),
                  reads=[dg, a16], writes=([R] if q == 0 else []), mark=True)
        R.buf.w = ("pe", tr.cnt["pe"])
        return R

    def ln_stat_act(self, j, acc):
        tr = self.tr
        sq = self.r_sqL.next()
        tr.op("act", lambda E: E.activation(out=sq.ap, in_=acc.ap, func=AF.Square), reads=[acc], writes=[sq])
        ab16 = self.r_accb.next()
        tr.op("act", lambda E: E.activation(out=ab16.ap, in_=acc.ap, func=AF.Copy), reads=[acc], writes=[ab16])
        return (j, ab16, sq)

    def ln_m3_m4(self, tile, h, h_next, side=None):
        tr = self.tr
        u = self.b_uM
        V = self.vecs

        def g_groups(i):
            sbuf, st = self.slab(tile, "m3", 2 * i + 1)
            res = []
            for ci in (0, 1):
                ps = self.r_ps.next()
                self.mm_group(ps, [(st[:, (ci * 8 + k) * 128:(ci * 8 + k + 1) * 128], sbuf, u[k]) for k in range(KC)])
                th = self.r_th.next()
                tr.op("act", lambda E, th=th, ps=ps: E.activation(out=th.ap, in_=ps.ap, func=AF.Tanh, scale=0.5),
                      reads=[ps], writes=[th])
                res.append(th)
            return res
        if side is not None:
            for _ in range(HOLD_A):
                if next(side, "end") == "end":
                    break
        pend_mm = self.ln_stat_act(*self._m1_pend)
        self.ln_stat_mm(*pend_mm)
        ths = {0: g_groups(0)}
        mean = self.b_mean
        tr.op("act", lambda E: E.activation(out=mean.ap, in_=self.b_S0.ap, func=AF.Copy), reads=[self.b_S0], writes=[mean])
        m2 = self.r_f.next()
        tr.op("dve", lambda E: E.tensor_tensor(out=m2.ap, in0=mean.ap, in1=mean.ap, op=ALU.mult), reads=[mean], writes=[m2])
        var = self.r_f.next()
        tr.op("dve", lambda E: E.tensor_tensor(out=var.ap, in0=self.b_S1.ap, in1=m2.ap, op=ALU.subtract),
              reads=[self.b_S1, m2], writes=[var])
        var2 = self.r_f.next()
        tr.op("dve", lambda E: E.tensor_scalar_max(out=var2.ap, in0=var.ap, scalar1=0.0), reads=[var], writes=[var2])
        sd = self.r_f.next()
        tr.op("act", lambda E: E.activation(out=sd.ap, in_=var2.ap, func=AF.Sqrt, bias=EPS, scale=1.0), reads=[var2], writes=[sd])
        lnr = self.b_lnr
        tr.op("dve", lambda E: E.reciprocal(out=lnr.ap, in_=sd.ap), reads=[sd], writes=[lnr])
        nmr = self.b_nmr
        tr.op("dve", lambda E: E.scalar_tensor_tensor(out=nmr.ap, in0=mean.ap, scalar=-1.0, in1=lnr.ap,
                                                      op0=ALU.mult, op1=ALU.mult), reads=[mean, lnr], writes=[nmr])
        xns = []
        for j in range(KC + 1):
            if j < KC:
                xn = self.r_f.next()
                acc = self.b_acc[j]
                tr.op("dve", lambda E, xn=xn, acc=acc: E.tensor_tensor(out=xn.ap, in0=acc.ap, in1=lnr.ap, op=ALU.mult),
                      reads=[acc, lnr], writes=[xn])
                xns.append(xn)
            if j >= 1:
                jj = j - 1
                xn = xns[jj]
                xo = self.r_f.next()
                tr.op("dve", lambda E, xn=xn, xo=xo: E.tensor_tensor(out=xo.ap, in0=xn.ap, in1=nmr.ap, op=ALU.add),
                      reads=[xn, nmr], writes=[xo])
                tr.op("act", lambda E, xo=xo, jj=jj: E.activation(
                    out=self.b_aact[jj].ap, in_=xo.ap, func=AF.Silu, bias=V[:, V_ALNB + jj:V_ALNB + jj + 1],
                    scale=V[:, V_ALNG + jj:V_ALNG + jj + 1]), reads=[xo, self.b_vecs], writes=[self.b_aact[jj]])
        if side is not None:
            for _ in side:
                pass
        ths[1] = g_groups(1)
        prev = None
        for i in range(KC):
            sbuf, st = self.slab(tile, "m3", 2 * i)
            tha, thb = ths.pop(i)
            ps_ya = self.r_ps.next()
            self.mm_group(ps_ya, [(st[:, (0 * 8 + k) * 128:(0 * 8 + k + 1) * 128], sbuf, self.b_aact[k]) for k in range(KC)])
            ps_yb = self.r_ps.next()
            self.mm_group(ps_yb, [(st[:, (1 * 8 + k) * 128:(1 * 8 + k + 1) * 128], sbuf, self.b_bact[k]) for k in range(KC)])
            t1 = self.r_f.next()
            tr.op("dve", lambda E, t1=t1, tha=tha, ps=ps_ya: E.scalar_tensor_tensor(
                out=t1.ap, in0=tha.ap, scalar=1.0, in1=ps.ap, op0=ALU.add, op1=ALU.mult), reads=[tha, ps_ya], writes=[t1])
            t2 = self.r_f.next()
            tr.op("dve", lambda E, t2=t2, thb=thb, ps=ps_yb: E.scalar_tensor_tensor(
                out=t2.ap, in0=thb.ap, scalar=1.0, in1=ps.ap, op0=ALU.add, op1=ALU.mult), reads=[thb, ps_yb], writes=[t2])
            if prev is not None:
                p1, p2, pi = prev
                tr.op("dve", lambda E, p1=p1, p2=p2, pi=pi: E.tensor_tensor(out=self.b_mm[pi].ap, in0=p1.ap, in1=p2.ap, op=ALU.add),
                      reads=[p1, p2], writes=[self.b_mm[pi]])
            prev = (t1, t2, i)
            if i + 2 < KC:
                ths[i + 2] = g_groups(i + 2)
        p1, p2, pi = prev
        tr.op("dve", lambda E: E.tensor_tensor(out=self.b_mm[pi].ap, in0=p1.ap, in1=p2.ap, op=ALU.add),
              reads=[p1, p2], writes=[self.b_mm[pi]])
        if h_next is not None:
            self.mixer_norm(h_next)
        for i in range(KC):
            sbuf, st = self.slab(tile, "wo", i // 2)
            ci = i % 2
            ps = self.r_ps.next()
            self.mm_group(ps, [(st[:, (ci * 8 + k) * 128:(ci * 8 + k + 1) * 128], sbuf, self.b_mm[k]) for k in range(KC)])
            tr.op("dve", lambda E, i=i, ps=ps: E.scalar_tensor_tensor(
                out=h[i].ap, in0=ps.ap, scalar=0.5, in1=h[i].ap, op0=ALU.mult, op1=ALU.add),
                reads=[ps, h[i]], writes=[h[i]])
        self.ffn_norm(h, 2)

    def final_store(self, tile, h):
        tr = self.tr
        rstd = self.norm_stats(h, self.b_rstdM)
        for k in range(KC):
            tr.op("dve", lambda E, k=k: E.scalar_tensor_tensor(
                out=h[k].ap, in0=h[k].ap, scalar=self.vecs[:, V_FINAL + k:V_FINAL + k + 1],
                in1=rstd.ap, op0=ALU.mult, op1=ALU.mult), reads=[h[k], rstd, self.b_vecs], writes=[h[k]])
        p = tile % 2
        dst = self.outT.rearrange("(k p) t -> p k t", p=128)[:, :, tile * T:(tile + 1) * T]
        self.tr.dma("pool", "ost%d" % p, self.b_out, h[0], dst, self.hT[p][:, :, :], extra_reads=h[1:])


_CACHE = {}


def _get_nc():
    if "nc" not in _CACHE:
        _CACHE["nc"] = Builder().build()
    return _CACHE["nc"]


def kernel(**inputs):
    inp = {k: np.asarray(v) for k, v in inputs.items()}
    x = np.asarray(inp["x"], np.float32)
    wpack = pack_weights(inp)
    vecs = pack_vecs(inp)
    nc = Builder().build()
    in_maps = []
    for c in range(N_CORES):
        in_maps.append({"xT": np.ascontiguousarray(x[c].T), "wpack": wpack, "vecs": vecs})
    res = run_bass_kernel_spmd(nc, in_maps, core_ids=list(range(N_CORES)))
    out = np.empty((N_CORES, SEQ, D), np.float32)
    for c in range(N_CORES):
        out[c] = np.asarray(res.results[c]["outT"]).T
    return out
```
